# Optimizing a Trainium2 kernel written in Bass

```python
import jax, jax.numpy as jnp
from jax import lax
import numpy as np

D_MODEL = 4096
BATCH = 1
SEQ = 16384
DEPTH = 1

MIX_DIM = D_MODEL
RET_DIM = MIX_DIM // 2
POOL_DIM = MIX_DIM - RET_DIM
RET_HEADS = 8
RET_HEAD_DIM = RET_DIM // RET_HEADS
POOL_WINDOWS = (2, 4, 8, 16)
POOL_GROUPS = len(POOL_WINDOWS)
POOL_GROUP_DIM = POOL_DIM // POOL_GROUPS
PROJ_DIM = 4 * RET_DIM + POOL_DIM
D_FF = ((8 * D_MODEL // 3 + 255) // 256) * 256
CHUNK = 128
ROPE_BASE = 10000.0
EPS = 1e-6

kernel_name = 'hybrid_retention_pool_macaron_encoder'


def _rmsnorm(x, gain):
    xf = x.astype(jnp.float32)
    y = xf * lax.rsqrt(jnp.mean(xf * xf, axis=-1, keepdims=True) + EPS)
    return (y * gain.astype(jnp.float32)).astype(x.dtype)


def _swiglu(h, w_gate, w_up, w_down):
    return (jax.nn.silu(h @ w_gate) * (h @ w_up)) @ w_down


def _rotary(t, positions):
    d = t.shape[-1]
    freqs = 1.0 / (ROPE_BASE ** (jnp.arange(0, d, 2, dtype=jnp.float32) / d))
    ang = positions.astype(jnp.float32)[..., None] * freqs
    cos = jnp.cos(ang)[:, :, None, :]
    sin = jnp.sin(ang)[:, :, None, :]
    tf = t.astype(jnp.float32)
    t1, t2 = tf[..., : d // 2], tf[..., d // 2:]
    return jnp.concatenate([t1 * cos - t2 * sin, t1 * sin + t2 * cos], axis=-1)


def _retention_one_direction(q, k, v, log_gamma, include_diag):
    b, h, s, d = q.shape
    n_chunks = s // CHUNK

    def to_chunks(t):
        return t.reshape(b, h, n_chunks, CHUNK, t.shape[-1]).transpose(2, 0, 1, 3, 4)

    idx = jnp.arange(CHUNK, dtype=jnp.float32)
    rel = idx[:, None] - idx[None, :]
    mask = (rel >= 0) if include_diag else (rel > 0)
    lg = log_gamma[:, None, None]
    intra_decay = jnp.where(mask[None], jnp.exp(jnp.where(mask[None], rel[None], 0.0) * lg), 0.0)
    q_decay = jnp.exp((idx + 1.0)[None, :] * log_gamma[:, None])[..., None]
    k_decay = jnp.exp((CHUNK - 1.0 - idx)[None, :] * log_gamma[:, None])[..., None]
    chunk_decay = jnp.exp(CHUNK * log_gamma)[:, None, None]

    def step(state, qkv):
        qc, kc, vc = qkv
        scores = jnp.einsum('bhid,bhjd->bhij', qc, kc) * intra_decay
        inner = jnp.einsum('bhij,bhje->bhie', scores, vc)
        cross = jnp.einsum('bhid,bhde->bhie', qc * q_decay, state)
        state = state * chunk_decay + jnp.einsum('bhjd,bhje->bhde', kc * k_decay, vc)
        return state, inner + cross

    state0 = jnp.zeros((b, h, d, v.shape[-1]), jnp.float32)
    _, out = lax.scan(step, state0, (to_chunks(q), to_chunks(k), to_chunks(v)))
    return out.transpose(1, 2, 0, 3, 4).reshape(b, h, s, v.shape[-1])


def _bidirectional_retention(q, k, v, logit_fwd, logit_bwd):
    lg_f = jax.nn.log_sigmoid(logit_fwd.astype(jnp.float32))
    lg_b = jax.nn.log_sigmoid(logit_bwd.astype(jnp.float32))
    out_f = _retention_one_direction(q, k, v, lg_f, True)
    flip = lambda t: t[:, :, ::-1]
    out_b = flip(_retention_one_direction(flip(q), flip(k), flip(v), lg_b, False))
    return out_f + out_b


def _centred_mean_minus_self(p, window):
    b, s, c = p.shape
    cs = jnp.concatenate([jnp.zeros((b, 1, c), p.dtype), jnp.cumsum(p, axis=1)], axis=1)
    lo = window // 2
    hi = window - 1 - lo
    n = jnp.arange(s)
    start = jnp.clip(n - lo, 0, s)
    end = jnp.clip(n + hi + 1, 0, s)
    total = cs[:, end] - cs[:, start]
    count = (end - start).astype(p.dtype)
    return total / count[None, :, None] - p


def setup_inputs(seed: int = 0) -> dict:
    key = jax.random.key(seed)
    ks = jax.random.split(key, 24)
    f32 = jnp.float32

    def nrm(k, shape, fan_in):
        return jax.random.normal(k, shape, f32) * (fan_in ** -0.5)

    def gain(k, shape):
        return 1.0 + 0.02 * jax.random.normal(k, shape, f32)

    hh = jnp.arange(RET_HEADS, dtype=f32)
    gamma = 1.0 - jnp.exp2(-5.0 - hh)
    base_logit = jnp.log(gamma) - jnp.log1p(-gamma)

    x = jax.random.normal(ks[0], (BATCH, SEQ, D_MODEL), f32)
    positions = jnp.broadcast_to(jnp.arange(SEQ, dtype=jnp.int32)[None, :], (BATCH, SEQ))
    return {
        'x': x,
        'positions': positions,
        'ffn1_norm': gain(ks[1], (DEPTH, D_MODEL)),
        'ffn1_w_gate': nrm(ks[2], (DEPTH, D_MODEL, D_FF), D_MODEL),
        'ffn1_w_up': nrm(ks[3], (DEPTH, D_MODEL, D_FF), D_MODEL),
        'ffn1_w_down': nrm(ks[4], (DEPTH, D_FF, D_MODEL), D_FF),
        'mix_norm': gain(ks[5], (DEPTH, D_MODEL)),
        'w_in': nrm(ks[6], (DEPTH, D_MODEL, PROJ_DIM), D_MODEL),
        'ret_decay_fwd': base_logit[None] + 0.1 * jax.random.normal(ks[7], (DEPTH, RET_HEADS), f32),
        'ret_decay_bwd': base_logit[None] + 0.1 * jax.random.normal(ks[8], (DEPTH, RET_HEADS), f32),
        'ret_head_norm': gain(ks[9], (DEPTH, RET_DIM)),
        'pool_w': nrm(ks[10], (DEPTH, POOL_GROUPS, POOL_GROUP_DIM, POOL_GROUP_DIM), POOL_GROUP_DIM),
        'pool_scale': gain(ks[11], (DEPTH, POOL_DIM)),
        'w_out': nrm(ks[12], (DEPTH, MIX_DIM, D_MODEL), MIX_DIM),
        'ffn2_norm': gain(ks[13], (DEPTH, D_MODEL)),
        'ffn2_w_gate': nrm(ks[14], (DEPTH, D_MODEL, D_FF), D_MODEL),
        'ffn2_w_up': nrm(ks[15], (DEPTH, D_MODEL, D_FF), D_MODEL),
        'ffn2_w_down': nrm(ks[16], (DEPTH, D_FF, D_MODEL), D_FF),
        'final_norm': gain(ks[17], (D_MODEL,)),
    }


def reference(x, positions, ffn1_norm, ffn1_w_gate, ffn1_w_up, ffn1_w_down, mix_norm, w_in,
              ret_decay_fwd, ret_decay_bwd, ret_head_norm, pool_w, pool_scale, w_out,
              ffn2_norm, ffn2_w_gate, ffn2_w_up, ffn2_w_down, final_norm):
    b, s, _ = x.shape
    for layer in range(DEPTH):
        x = x + 0.5 * _swiglu(_rmsnorm(x, ffn1_norm[layer]), ffn1_w_gate[layer],
                              ffn1_w_up[layer], ffn1_w_down[layer])

        h = _rmsnorm(x, mix_norm[layer])
        z = h @ w_in[layer]
        q, k, v, g, p = jnp.split(z, [RET_DIM, 2 * RET_DIM, 3 * RET_DIM, 4 * RET_DIM], axis=-1)

        q = _rotary(q.reshape(b, s, RET_HEADS, RET_HEAD_DIM), positions)
        k = _rotary(k.reshape(b, s, RET_HEADS, RET_HEAD_DIM), positions) * (RET_HEAD_DIM ** -0.5)
        v = v.reshape(b, s, RET_HEADS, RET_HEAD_DIM).astype(jnp.float32)
        ret = _bidirectional_retention(q.transpose(0, 2, 1, 3), k.transpose(0, 2, 1, 3),
                                       v.transpose(0, 2, 1, 3),
                                       ret_decay_fwd[layer], ret_decay_bwd[layer])
        ret = ret.transpose(0, 2, 1, 3)
        ret = ret * lax.rsqrt(jnp.mean(ret * ret, axis=-1, keepdims=True) + EPS)
        ret = ret.reshape(b, s, RET_DIM) * ret_head_norm[layer].astype(jnp.float32)
        ret = (ret * jax.nn.silu(g.astype(jnp.float32))).astype(x.dtype)

        pf = p.astype(jnp.float32).reshape(b, s, POOL_GROUPS, POOL_GROUP_DIM)
        pooled = jnp.stack([_centred_mean_minus_self(pf[:, :, gi], w)
                            for gi, w in enumerate(POOL_WINDOWS)], axis=2)
        pool_out = jnp.einsum('bsgc,gcd->bsgd', pooled, pool_w[layer].astype(jnp.float32))
        pool_out = (pool_out.reshape(b, s, POOL_DIM) * pool_scale[layer].astype(jnp.float32)).astype(x.dtype)

        x = x + jnp.concatenate([ret, pool_out], axis=-1) @ w_out[layer]

        x = x + 0.5 * _swiglu(_rmsnorm(x, ffn2_norm[layer]), ffn2_w_gate[layer],
                              ffn2_w_up[layer], ffn2_w_down[layer])
    return _rmsnorm(x, final_norm)
```

```python
import numpy as np
import ml_dtypes
import concourse.bass as bass
import concourse.mybir as mybir
from concourse.bass_utils import run_bass_kernel_spmd

F32 = mybir.dt.float32
BF16 = mybir.dt.bfloat16
I32 = mybir.dt.int32
AF = mybir.ActivationFunctionType
ALU = mybir.AluOpType

NCORES = 8
EPS = 1e-6
NDS = 24
TWO_PI = 6.283185307179586
CW1 = 6.28125
CW2 = TWO_PI - CW1
MAGIC = 12582912.0
PI_SAFE = 3.1415925


class Cfg:
    def __init__(self, D, FF, SEQ, H):
        self.D, self.FF, self.SEQ, self.H = D, FF, SEQ, H
        self.T = SEQ // NCORES
        self.TT = 512
        self.NT = self.T // self.TT
        self.SUB = self.TT // 128
        self.KC = D // 128
        self.FC = FF // 128
        self.RET = D // 2
        self.PD = D - self.RET
        self.RC = self.RET // 128
        self.PC = self.PD // 128
        self.PG = self.PD // 4
        self.PGC = self.PG // 128
        self.MC = self.RC + self.PC
        self.OG = D // 512
        self.PROJ = 4 * self.RET + self.PD
        self.NCH = self.T // 128
        self.VG = (2 * self.RET) // 512
        assert self.RET == H * 256
        c = 0
        self.C_RELP = c; c += 128
        self.C_RELN = c; c += 128
        self.C_MGE = c; c += 128
        self.C_MGT = c; c += 128
        self.C_IP1 = c; c += 128
        self.C_128MI = c; c += 128
        self.C_IDENT = c; c += 128
        self.C_127MJ = c; c += 1
        self.C_J = c; c += 1
        self.C_FREQ = c; c += 1
        self.C_VL = c; c += 1
        self.C_VR = c; c += 1
        self.C_MASKL = c; c += 8
        self.C_MASKR = c; c += 8
        self.C_EXPF = c; c += 8
        self.C_MSKF = c; c += 8
        self.C_EXPB = c; c += 8
        self.C_MSKB = c; c += 8
        self.NCST = c
        s = 0
        self.S_G1 = s; s += self.KC
        self.S_GM = s; s += self.KC
        self.S_G2 = s; s += self.KC
        self.S_PSC = s; s += self.PC
        self.S_LF = s; s += H
        self.S_LB = s; s += H
        self.NSP = s
        self.CCW = 2 * H * 512 + self.PC * 16


class Op:
    __slots__ = ("idx", "eng", "fn", "deps", "dma", "sig", "sigval", "dma_n", "cc", "strict")

    def __init__(self, idx, eng, fn, deps, dma, cc):
        self.idx, self.eng, self.fn, self.deps, self.dma, self.cc = idx, eng, fn, deps, dma, cc
        self.sig = False
        self.sigval = 0
        self.dma_n = -1
        self.strict = False


class Prog:
    ENGS = ("pe", "act", "dve", "pool", "sp")

    def __init__(self):
        self.ops = []
        self.lastw = {}
        self.readers = {}
        self.pending = {e: None for e in self.ENGS}
        self.last_on = {e: None for e in self.ENGS}
        self.dma_ops = []

    def add(self, eng, fn, reads=(), writes=(), dma=False, cc=False, strict=False):
        deps = set()
        if strict and self.last_on[eng] is not None:
            deps.add(self.last_on[eng])
        pr = [k for k in reads if isinstance(k, tuple) and k[0] in ("ps", "psb")]
        if pr and eng != "pe":
            writes = list(writes) + pr
        for k in reads:
            w = self.lastw.get(k)
            if w is not None:
                deps.add(w)
        for k in writes:
            w = self.lastw.get(k)
            if w is not None:
                deps.add(w)
            for r in self.readers.get(k, ()):
                deps.add(r)
        if self.pending[eng] is not None:
            deps |= self.pending[eng]
            self.pending[eng] = None
        idx = len(self.ops)
        op = Op(idx, eng, fn, deps, dma, cc)
        op.strict = strict
        if dma:
            n = len(self.dma_ops)
            if n >= NDS:
                deps.add(self.dma_ops[n - NDS].idx)
            op.dma_n = n
            self.dma_ops.append(op)
        self.ops.append(op)
        self.last_on[eng] = idx
        for k in reads:
            self.readers.setdefault(k, []).append(idx)
        for k in writes:
            self.lastw[k] = idx
            self.readers[k] = []
        return idx

    def barrier(self):
        b = set(v for v in self.last_on.values() if v is not None)
        for op in self.dma_ops[-NDS:]:
            b.add(op.idx)
        for e in self.ENGS:
            self.pending[e] = set(b) if self.pending[e] is None else (self.pending[e] | b)

    def emit(self, nc, block, esem, dsems, ccsem):
        ops = self.ops
        for op in ops:
            for d in op.deps:
                D = ops[d]
                if D.dma or D.cc:
                    continue
                if D.eng == "pe" and op.eng == "pe" and not op.dma and not op.strict:
                    continue
                D.sig = True
        cnt = {e: 0 for e in self.ENGS}
        for op in ops:
            if op.sig and not op.dma and not op.cc:
                cnt[op.eng] += 1
                op.sigval = cnt[op.eng]
        ncc = [0]
        ccval = {}
        for op in ops:
            if op.cc:
                ncc[0] += 1
                ccval[op.idx] = ncc[0]

        def body_for(e):
            mine = [op for op in ops if op.eng == e]

            def body(eng):
                waited = {}
                for op in mine:
                    need = {}
                    for d in op.deps:
                        D = ops[d]
                        if D.dma:
                            key = ("d", D.dma_n % NDS)
                            sem = dsems[D.dma_n % NDS]
                            val = 16 * (D.dma_n // NDS + 1)
                        elif D.cc:
                            key = ("c",)
                            sem = ccsem
                            val = ccval[D.idx]
                        else:
                            if D.eng == "pe" and e == "pe" and not op.dma and not op.strict:
                                continue
                            key = ("e", D.eng)
                            sem = esem[D.eng]
                            val = D.sigval
                        if key not in need or need[key][1] < val:
                            need[key] = (sem, val)
                    for key, (sem, val) in need.items():
                        if waited.get(key, 0) < val:
                            eng.wait_ge(sem, val)
                            waited[key] = val
                    if op.fn is not None:
                        ins = op.fn(eng)
                        if op.dma:
                            ins.then_inc(dsems[op.dma_n % NDS], 16)
                        elif op.cc:
                            ins.then_inc(ccsem)
                        elif op.sig:
                            ins.then_inc(esem[e], 1)

            return body

        block.tensor(body_for("pe"))
        block.scalar(body_for("act"))
        block.vector(body_for("dve"))
        block.gpsimd(body_for("pool"))
        block.sync(body_for("sp"))


class Carver:
    def __init__(self, arena, start_bytes, limit_bytes):
        self.arena, self.off, self.limit = arena, start_bytes, limit_bytes

    def take(self, shape, dtype):
        esz = 2 if dtype == BF16 else 4
        n = 1
        for s in shape[1:]:
            n *= s
        nbytes = (n * esz + 31) // 32 * 32
        a = self.off
        self.off += nbytes
        assert self.off <= self.limit, ("arena overflow", self.off, self.limit)
        v = self.arena[:, a // 4:(a + n * esz + 3) // 4]
        if dtype != F32:
            v = v.bitcast(dtype)
            v = v[:, 0:n]
        if len(shape) == 2:
            return v
        names = " ".join("d%d" % i for i in range(len(shape) - 1))
        kw = {"d%d" % i: shape[i + 1] for i in range(len(shape) - 1)}
        return v.rearrange("p (%s) -> p %s" % (names, names), **kw)


def build(cfg):
    c = cfg
    nc = bass.Bass("TRN2", target_bir_lowering=False)
    D, FF, T, TT, H = c.D, c.FF, c.T, c.TT, c.H
    KC, FC, RC, PC, MC, OG, SUB, NT, NCH = c.KC, c.FC, c.RC, c.PC, c.MC, c.OG, c.SUB, c.NT, c.NCH
    RET, PD, PG, PGC = c.RET, c.PD, c.PG, c.PGC

    def din(name, shape, dt=F32):
        return nc.dram_tensor(name, shape, dt, kind="ExternalInput")

    x_d = din("x", [T, D])
    pos_d = din("pos", [1, T], I32)
    cst_d = din("cst", [128, c.NCST])
    sp_d = din("smallp", [128, c.NSP])
    hng_d = din("hn_gain", [1, RET])
    fin_d = din("fin_gain", [1, D])
    g1_d = din("g1_row", [1, D]); gm_d = din("gm_row", [1, D]); g2_d = din("g2_row", [1, D])
    w1g_d = din("w1g", [D, FF]); w1u_d = din("w1u", [D, FF]); w1d_d = din("w1d", [FF, D])
    w2g_d = din("w2g", [D, FF]); w2u_d = din("w2u", [D, FF]); w2d_d = din("w2d", [FF, D])
    win_d = din("w_in", [D, c.PROJ])
    wout_d = din("w_out", [D, D])
    pw_d = din("pool_w", [4 * PG, PG])
    out_d = nc.dram_tensor("out", [T, D], F32, kind="ExternalOutput")

    def scr(name, shape, dt):
        return nc.dram_tensor(name, shape, dt)

    wgs = [scr("s_wg%d" % i, [FC, 128, D], BF16) for i in range(2)]
    wus = [scr("s_wu%d" % i, [FC, 128, D], BF16) for i in range(2)]
    wds = [scr("s_wd%d" % i, [OG, 128, FC, 512], BF16) for i in range(2)]
    winA = scr("s_winA", [2 * RC + PC, 128, D], BF16)
    winB = scr("s_winB", [c.VG, 128, KC, 512], BF16)
    woutS = scr("s_wout", [OG, 128, MC, 512], BF16)
    pwS = scr("s_pw", [128, 4 * PGC * PG], BF16)
    x1_d = scr("s_x1", [T, D], F32)
    x2_d = scr("s_x2", [T, D], F32)
    qT_d = scr("s_qT", [RC, 128, T], BF16)
    kT_d = scr("s_kT", [RC, 128, T], BF16)
    v_d = scr("s_v", [T, RET], BF16)
    gg_d = scr("s_gg", [T, RET], F32)
    pT_d = scr("s_pT", [PC, 128, T + 16], F32)
    L_d = [scr("s_L%d" % i, [NCH, 128, H * 512], F32) for i in range(2)]
    cin_d = scr("s_cin", [2, 128, H * 512], F32)
    ccin_d = scr("s_ccin", [128, c.CCW], F32)
    ccout_d = scr("s_ccout", [NCORES * 128, c.CCW], F32)

    import os
    KSUB = int(os.environ.get('K_SUB', '99'))
    KSW = int(os.environ.get('K_SW', '99'))
    P = Prog()

    PERS = 20 * 1024
    ARENA_BYTES = 207 * 1024
    arena = nc.alloc_sbuf_tensor("arena", [128, ARENA_BYTES // 4], F32)
    psf = nc.alloc_psum_tensor("psf", [128, 6, 512], F32)
    psb = nc.alloc_psum_tensor("psb", [128, 2, 1024], BF16)

    pers = Carver(arena, 0, PERS)
    cst = pers.take([128, c.NCST], F32)
    smp = pers.take([128, c.NSP], F32)
    ident_f = cst[:, c.C_IDENT:c.C_IDENT + 128]
    ident_b = pers.take([128, 128], BF16)
    lg = [pers.take([128, H], F32) for _ in range(2)]
    DT = pers.take([128, H, 128], F32)
    qd = [pers.take([128, H, 128], F32) for _ in range(2)]
    kd = [pers.take([128, H], F32) for _ in range(2)]
    gpow = [pers.take([128, NCH + 1, H], F32) for _ in range(2)]
    coef = [pers.take([128, 8, H], F32) for _ in range(2)]
    ptmp = pers.take([128, 128], F32)
    ptmp2 = pers.take([128, 128], F32)

    def col(i):
        return cst[:, i:i + 1]

    def stage(limit=ARENA_BYTES):
        P.barrier()
        return Carver(arena, PERS, limit)

    def dma(out, in_, reads=(), writes=()):
        P.add("sp", lambda e, o=out, i=in_: e.dma_start(out=o, in_=i), reads, writes, dma=True)

    def act_fn(out, in_, func, reads, writes, scale=1.0, bias=0.0, accum=None):
        if func == AF.Copy and not isinstance(scale, float):
            func = AF.Identity
        if accum is None:
            P.add("act", lambda e, o=out, i=in_: e.activation(out=o, in_=i, func=func, bias=bias, scale=scale),
                  reads, writes)
        else:
            P.add("act", lambda e, o=out, i=in_: e.activation(out=o, in_=i, func=func, bias=bias, scale=scale,
                                                             accum_out=accum), reads, writes)

    def ts_op(eng, out, in0, s1, s2, op0, op1, reads, writes):
        if s2 is None:
            P.add(eng, lambda e: e.tensor_scalar(out=out, in0=in0, scalar1=s1, scalar2=None, op0=op0), reads, writes)
        else:
            P.add(eng, lambda e: e.tensor_scalar(out=out, in0=in0, scalar1=s1, scalar2=s2, op0=op0, op1=op1),
                  reads, writes)

    def tt_op(eng, out, in0, in1, op, reads, writes):
        P.add(eng, lambda e: e.tensor_tensor(out=out, in0=in0, in1=in1, op=op), reads, writes)

    def stt_op(out, in0, scalar, in1, op0, op1, reads, writes):
        P.add("dve", lambda e: e.scalar_tensor_tensor(out=out, in0=in0, scalar=scalar, in1=in1, op0=op0, op1=op1),
              reads, writes)

    def copy_op(eng, out, in_, reads, writes):
        if eng == "act":
            act_fn(out, in_, AF.Copy, reads, writes)
        else:
            P.add(eng, lambda e: e.tensor_copy(out=out, in_=in_), reads, writes)

    def mm(out, lhsT, rhs, start, stop, reads, writes):
        P.add("pe", lambda e: e.matmul(out, lhsT, rhs, start=start, stop=stop), reads, writes)

    def tr(out, in_, ident, reads, writes, strict=False):
        P.add("pe", lambda e: e.transpose(out, in_, ident), reads, writes, strict=strict)

    def rstd_from_ss(ss, tmp, n, inv_n, key):
        ts_op("dve", tmp, ss, inv_n, EPS, ALU.mult, ALU.add, [key], [key + ("t",)])
        act_fn(tmp, tmp, AF.Sqrt, [key + ("t",)], [key + ("t",)])
        P.add("dve", lambda e: e.reciprocal(out=ss, in_=tmp), [key + ("t",)], [key])

    dma(cst, cst_d.ap(), [], ["cst"])
    dma(smp, sp_d.ap(), [], ["smp"])
    copy_op("dve", ident_b, ident_f, ["cst"], ["identb"])
    for d_ in range(2):
        src = smp[:, (c.S_LF if d_ == 0 else c.S_LB):(c.S_LF if d_ == 0 else c.S_LB) + H]
        k_ = ("lg", d_)
        act_fn(lg[d_], src, AF.Exp, ["smp"], [k_], scale=-1.0)
        ts_op("dve", lg[d_], lg[d_], 1.0, None, ALU.add, None, [k_], [k_])
        act_fn(lg[d_], lg[d_], AF.Ln, [k_], [k_])
        ts_op("dve", lg[d_], lg[d_], -1.0, None, ALU.mult, None, [k_], [k_])
    relp = cst[:, c.C_RELP:c.C_RELP + 128]
    reln = cst[:, c.C_RELN:c.C_RELN + 128]
    mge = cst[:, c.C_MGE:c.C_MGE + 128]
    mgt = cst[:, c.C_MGT:c.C_MGT + 128]
    for h in range(H):
        act_fn(ptmp, relp, AF.Exp, [("lg", 0), "cst"], ["ptmp"], scale=lg[0][:, h:h + 1])
        tt_op("dve", ptmp, ptmp, mge, ALU.mult, ["ptmp"], ["ptmp"])
        act_fn(ptmp2, reln, AF.Exp, [("lg", 1), "cst"], ["ptmp2"], scale=lg[1][:, h:h + 1])
        tt_op("dve", ptmp2, ptmp2, mgt, ALU.mult, ["ptmp2"], ["ptmp2"])
        tt_op("dve", DT[:, h, :], ptmp, ptmp2, ALU.add, ["ptmp", "ptmp2"], ["DT"])
        act_fn(qd[0][:, h, :], cst[:, c.C_IP1:c.C_IP1 + 128], AF.Exp, [("lg", 0), "cst"], ["qd"],
               scale=lg[0][:, h:h + 1])
        act_fn(qd[1][:, h, :], cst[:, c.C_128MI:c.C_128MI + 128], AF.Exp, [("lg", 1), "cst"], ["qd"],
               scale=lg[1][:, h:h + 1])
    act_fn(kd[0], lg[0], AF.Exp, [("lg", 0), "cst"], ["kd"], scale=col(c.C_127MJ))
    act_fn(kd[1], lg[1], AF.Exp, [("lg", 1), "cst"], ["kd"], scale=col(c.C_J))
    for d_ in range(2):
        for n in range(NCH + 1):
            act_fn(gpow[d_][:, n, :], lg[d_], AF.Exp, [("lg", d_)], ["gpow"], scale=128.0 * n)
        ecol = c.C_EXPF if d_ == 0 else c.C_EXPB
        mcol = c.C_MSKF if d_ == 0 else c.C_MSKB
        for j in range(8):
            act_fn(coef[d_][:, j, :], lg[d_], AF.Exp, [("lg", d_), "cst"], [("coef", d_, j)], scale=col(ecol + j))
            ts_op("dve", coef[d_][:, j, :], coef[d_][:, j, :], col(mcol + j), None, ALU.mult, None,
                  [("coef", d_, j)], [("coef", d_, j)])

    cast_rr = [0]

    def cast(out, in_, reads, writes):
        e = ("dve", "act")[cast_rr[0] % 2]
        cast_rr[0] += 1
        copy_op(e, out, in_, reads, writes)

    def convert_A(src_ap, col0, ncols, dst, oc0):
        car = stage()
        K = src_ap.shape[0]
        kcn = K // 128
        inb = [car.take([128, kcn, 256], F32) for _ in range(2)]
        ob = [car.take([128, 2, kcn * 128], BF16) for _ in range(2)]
        srcv = src_ap.rearrange("(kc p) n -> p kc n", p=128)
        for i in range(ncols // 256):
            b = i % 2
            dma(inb[b], srcv[:, :, col0 + i * 256: col0 + (i + 1) * 256], [], [("cin", b)])
            for f in range(2):
                cast(ob[b][:, f, :].rearrange("p (kc m) -> p kc m", m=128), inb[b][:, :, f * 128:(f + 1) * 128],
                     [("cin", b)], [("cout", b, f)])
            dma(dst[oc0 + 2 * i: oc0 + 2 * i + 2].rearrange("f p x -> p f x"), ob[b],
                [("cout", b, 0), ("cout", b, 1)], [])

    def convert_B(src_ap, col0, nog, dst):
        car = stage()
        K = src_ap.shape[0]
        kcn = K // 128
        inb = [car.take([128, 8, 512], F32) for _ in range(2)]
        ob = [car.take([128, 8, 512], BF16) for _ in range(2)]
        srcv = src_ap.rearrange("(kc p) n -> p kc n", p=128)
        i = 0
        for og in range(nog):
            k0 = 0
            while k0 < kcn:
                kn = min(8, kcn - k0)
                b = i % 2
                i += 1
                dma(inb[b][:, 0:kn, :], srcv[:, k0:k0 + kn, col0 + og * 512: col0 + (og + 1) * 512], [], [("cin", b)])
                cast(ob[b][:, 0:kn, :], inb[b][:, 0:kn, :], [("cin", b)], [("cout", b)])
                dma(dst[og, :, k0:k0 + kn, :], ob[b][:, 0:kn, :], [("cout", b)], [])
                k0 += kn

    for i, (g_, u_, d__) in enumerate(((w1g_d, w1u_d, w1d_d), (w2g_d, w2u_d, w2d_d))):
        convert_A(g_.ap(), 0, FF, wgs[i], 0)
        convert_A(u_.ap(), 0, FF, wus[i], 0)
        convert_B(d__.ap(), 0, OG, wds[i])
    convert_A(win_d.ap(), 0, RET, winA, 0)
    convert_A(win_d.ap(), RET, RET, winA, RC)
    convert_A(win_d.ap(), 4 * RET, PD, winA, 2 * RC)
    convert_B(win_d.ap(), 2 * RET, c.VG, winB)
    convert_B(wout_d.ap(), 0, OG, woutS)
    car = stage()
    pin = car.take([128, 4 * PGC, PG], F32)
    pob = car.take([128, 4 * PGC, PG], BF16)
    dma(pin, pw_d.ap().rearrange("(gc p) d -> p gc d", p=128), [], ["pin"])
    cast(pob, pin, ["pin"], ["pob"])
    dma(pwS.ap().rearrange("p (a d) -> p a d", d=PG), pob, ["pob"], [])

    def norm_scratch(car, region=None):
        if region is not None and region.shape[1] >= 6 * D:
            r = region
        else:
            r = car.take([128, 6 * D], BF16)
        return (r[:, 0:D], r[:, D:3 * D].bitcast(F32), r[:, 3 * D:4 * D], r[:, 4 * D:6 * D].bitcast(F32))

    def norm_to_hT(car, src_d, t, grow_d, hT, scr4):
        junk, xrow, xsb, gbc = scr4
        ss = car.take([128, SUB], F32)
        sst = car.take([128, SUB], F32)
        dma(gbc, grow_d.ap().partition_broadcast(128).rearrange("p o n -> p (o n)"), [], ["gbc"])
        for ts in range(SUB):
            r0 = t * TT + ts * 128
            dma(xrow, src_d[r0:r0 + 128, :], [], ["xrow"])
            act_fn(junk, xrow, AF.Square, ["xrow"], ["junk", ("ss", ts)], accum=ss[:, ts:ts + 1])
            rstd_from_ss(ss[:, ts:ts + 1], sst[:, ts:ts + 1], 1, 1.0 / D, ("ss", ts))
            stt_op(xsb, xrow, ss[:, ts:ts + 1], gbc, ALU.mult, ALU.mult, ["xrow", ("ss", ts), "gbc"], ["xsb"])
            if KSUB <= 1:
                continue
            for gi_, k0 in enumerate(range(0, KC, 8)):
                bk = gi_ % 2
                pk = ("psb", bk)
                kn = min(8, KC - k0)
                for sub in range(kn):
                    kc = k0 + sub
                    tr(psb[:, bk, sub * 128:(sub + 1) * 128], xsb[:, kc * 128:(kc + 1) * 128], ident_b,
                       ["xsb", "identb"], [pk])
                copy_op("act" if gi_ % 2 == 0 else "dve", hT[:, k0:k0 + kn, ts * 128:(ts + 1) * 128],
                        psb[:, bk, 0:kn * 128].rearrange("p (k m) -> p k m", m=128), [pk],
                        [("hT", kc) for kc in range(k0, k0 + kn)])

    NW = 6

    class WRing:
        def __init__(self, car, n=NW):
            self.bufs = [car.take([128, 4096], BF16) for _ in range(n)]
            self.i = 0

        def next(self):
            b = self.i % len(self.bufs)
            self.i += 1
            return b, self.bufs[b]

    def ffn_stage(t, src_d, grow_d, wi, dst_d):
        car = stage()
        hT = car.take([128, KC, TT], BF16)
        actreg = car.take([128, FC * TT], BF16)
        act = actreg.rearrange("p (a b) -> p a b", b=TT)
        ring = WRing(car)
        sg = [car.take([128, TT], F32) for _ in range(2)]
        xres = [car.take([128, 512], F32) for _ in range(2)]
        ost = [car.take([128, 512], F32) for _ in range(2)]
        norm_to_hT(car, src_d, t, grow_d, hT, norm_scratch(car, actreg))
        hkeys = [("hT", kc) for kc in range(KC)]
        if KSUB <= 2:
            return
        loads = []
        for fc in range(FC):
            loads.append((wgs[wi], fc))
            loads.append((wus[wi], fc))
        PF = NW - 2
        slots = {}

        def issue(n):
            if n < len(loads):
                w, fc = loads[n]
                b, buf = ring.next()
                dma(buf[:, 0:D], w[fc], [], [("wb", b)])
                slots[n] = (b, buf)

        for n in range(PF):
            issue(n)
        for fc in range(FC):
            for which in range(2):
                n = 2 * fc + which
                issue(n + PF)
                b, buf = slots.pop(n)
                bank = which * 2 + fc % 2
                for kc in range(KC):
                    mm(psf[:, bank, 0:TT], buf[:, kc * 128:(kc + 1) * 128], hT[:, kc, :], kc == 0, kc == KC - 1,
                       [("wb", b)] + (hkeys if kc == 0 else []), [("ps", bank)])
            gb, ub = fc % 2, 2 + fc % 2
            act_fn(sg[fc % 2], psf[:, gb, 0:TT], AF.Silu, [("ps", gb)], [("sg", fc % 2)])
            tt_op("dve", act[:, fc, :], sg[fc % 2], psf[:, ub, 0:TT], ALU.mult, [("sg", fc % 2), ("ps", ub)],
                  [("act", fc), "junk"])
        if KSUB <= 3:
            return
        pieces = []
        for g in range(OG):
            f0 = 0
            while f0 < FC:
                fn = min(8, FC - f0)
                pieces.append((g, f0, fn))
                f0 += fn
        slots = {}

        def issue_d(n):
            if n < len(pieces):
                g, f0, fn = pieces[n]
                b, buf = ring.next()
                bv = buf.rearrange("p (a b) -> p a b", b=512)
                dma(bv[:, 0:fn, :], wds[wi][g, :, f0:f0 + fn, :], [], [("wb", b)])
                slots[n] = (b, bv)

        for n in range(PF):
            issue_d(n)
        ev = 0
        for n, (g, f0, fn) in enumerate(pieces):
            issue_d(n + PF)
            b, bv = slots.pop(n)
            for fi in range(fn):
                f = f0 + fi
                for ts in range(SUB):
                    mm(psf[:, ts, :], act[:, f, ts * 128:(ts + 1) * 128], bv[:, fi, :], f == 0, f == FC - 1,
                       [("wb", b), ("act", f)], [("ps", ts)])
            if f0 + fn == FC:
                for ts in range(SUB):
                    r0 = t * TT + ts * 128
                    e = ev % 2
                    ev += 1
                    dma(xres[e], src_d[r0:r0 + 128, g * 512:(g + 1) * 512], [], [("xres", e)])
                    stt_op(ost[e], psf[:, ts, :], 0.5, xres[e], ALU.mult, ALU.add, [("ps", ts), ("xres", e)],
                           [("ost", e)])
                    dma(dst_d[r0:r0 + 128, g * 512:(g + 1) * 512], ost[e], [("ost", e)], [])

    def win_stage(t):
        car = stage()
        hT = car.take([128, KC, TT], BF16)
        ring = WRing(car)
        norm_to_hT(car, x1_d, t, gm_d, hT, norm_scratch(car))
        hkeys = [("hT", kc) for kc in range(KC)]
        posi = car.take([128, TT], I32)
        ang = car.take([128, TT], F32)
        kk = car.take([128, TT], F32)
        rr = car.take([128, TT], F32)
        msk = car.take([128, TT], F32)
        cosT = car.take([128, TT], F32)
        sinT = car.take([128, TT], F32)
        e12 = [[car.take([128, TT], F32) for _ in range(2)] for _ in range(2)]
        ta = car.take([128, TT], F32)
        tb = car.take([128, TT], F32)
        ob = [car.take([128, 2, TT], BF16) for _ in range(2)]
        pst = [car.take([128, TT], F32) for _ in range(2)]
        hng = car.take([128, RET], F32)
        vst = [car.take([128, 512], BF16) for _ in range(2)]
        gst = [car.take([128, 512], F32) for _ in range(2)]
        dma(hng, hng_d.ap().partition_broadcast(128).rearrange("p o n -> p (o n)"), [], ["hng"])
        dma(posi, pos_d[0:1, t * TT:(t + 1) * TT].partition_broadcast(128).rearrange("p o n -> p (o n)"), [], ["posi"])
        copy_op("dve", ang, posi, ["posi"], ["ang"])
        ts_op("dve", ang, ang, col(c.C_FREQ), None, ALU.mult, None, ["ang", "cst"], ["ang"])
        ts_op("dve", kk, ang, 1.0 / TWO_PI, MAGIC, ALU.mult, ALU.add, ["ang"], ["kk"])
        ts_op("dve", kk, kk, -MAGIC, None, ALU.add, None, ["kk"], ["kk"])
        stt_op(rr, kk, -CW1, ang, ALU.mult, ALU.add, ["kk", "ang"], ["rr"])
        stt_op(rr, kk, -CW2, rr, ALU.mult, ALU.add, ["kk", "rr"], ["rr"])
        ts_op("dve", rr, rr, PI_SAFE, -PI_SAFE, ALU.min, ALU.max, ["rr"], ["rr"])
        act_fn(sinT, rr, AF.Sin, ["rr"], ["sin"])
        ts_op("dve", kk, rr, np.pi / 2, None, ALU.add, None, ["rr", "kk"], ["kk"])
        ts_op("dve", msk, kk, np.pi, None, ALU.is_gt, None, ["kk"], ["msk"])
        stt_op(kk, msk, -TWO_PI, kk, ALU.mult, ALU.add, ["msk", "kk"], ["kk"])
        ts_op("dve", kk, kk, PI_SAFE, -PI_SAFE, ALU.min, ALU.max, ["kk"], ["kk"])
        act_fn(cosT, kk, AF.Sin, ["kk"], ["cos"])
        loadsA = list(range(2 * RC + PC))
        PF = NW - 2
        slots = {}

        def issue(n):
            if n < len(loadsA):
                b, buf = ring.next()
                dma(buf[:, 0:D], winA[loadsA[n]], [], [("wb", b)])
                slots[n] = (b, buf)

        for n in range(PF):
            issue(n)

        def chunk_mm(n, bank):
            issue(n + PF)
            b, buf = slots.pop(n)
            for kc in range(KC):
                mm(psf[:, bank, 0:TT], buf[:, kc * 128:(kc + 1) * 128], hT[:, kc, :], kc == 0, kc == KC - 1,
                   [("wb", b)] + (hkeys if kc == 0 else []), [("ps", bank)])

        for pair in range(RC):
            isk = pair >= RC // 2
            pb = pair % 2
            for which in range(2):
                bank = pb * 2 + which
                chunk_mm(2 * pair + which, bank)
                act_fn(e12[pb][which], psf[:, bank, 0:TT], AF.Copy, [("ps", bank)], [("e", pb, which)],
                       scale=(1.0 / 16.0 if isk else 1.0))
            e1, e2 = e12[pb]
            rk = [("e", pb, 0), ("e", pb, 1), "cos", "sin"]
            tt_op("dve", ta, e1, cosT, ALU.mult, rk, ["ta"])
            tt_op("dve", tb, e2, sinT, ALU.mult, rk, ["tb"])
            tt_op("dve", ob[pb][:, 0, :], ta, tb, ALU.subtract, ["ta", "tb"], [("ob", pb)])
            tt_op("dve", ta, e1, sinT, ALU.mult, rk, ["ta"])
            tt_op("dve", tb, e2, cosT, ALU.mult, rk, ["tb"])
            tt_op("dve", ob[pb][:, 1, :], ta, tb, ALU.add, ["ta", "tb"], [("ob", pb)])
            dst = (kT_d if isk else qT_d)
            cc0 = 2 * (pair - (RC // 2 if isk else 0))
            dma(dst[cc0:cc0 + 2].rearrange("c p t -> p c t")[:, :, t * TT:(t + 1) * TT], ob[pb], [("ob", pb)], [])
        for pc in range(PC):
            bank = pc % 2
            chunk_mm(2 * RC + pc, bank)
            act_fn(pst[pc % 2], psf[:, bank, 0:TT], AF.Copy, [("ps", bank)], [("pst", pc % 2)])
            dma(pT_d[pc, :, 8 + t * TT: 8 + (t + 1) * TT], pst[pc % 2], [("pst", pc % 2)], [])
        pieces = []
        for og in range(c.VG):
            k0 = 0
            while k0 < KC:
                kn = min(8, KC - k0)
                pieces.append((og, k0, kn))
                k0 += kn
        slots = {}

        def issue_b(n):
            if n < len(pieces):
                og, k0, kn = pieces[n]
                b, buf = ring.next()
                bv = buf.rearrange("p (a b) -> p a b", b=512)
                dma(bv[:, 0:kn, :], winB[og, :, k0:k0 + kn, :], [], [("wb", b)])
                slots[n] = (b, bv)

        for n in range(PF):
            issue_b(n)
        ev = 0
        for n, (og, k0, kn) in enumerate(pieces):
            issue_b(n + PF)
            b, bv = slots.pop(n)
            for ki in range(kn):
                kc = k0 + ki
                for ts in range(SUB):
                    mm(psf[:, ts, :], hT[:, kc, ts * 128:(ts + 1) * 128], bv[:, ki, :], kc == 0, kc == KC - 1,
                       [("wb", b), ("hT", kc)], [("ps", ts)])
            if k0 + kn == KC:
                isv = og < c.VG // 2
                for ts in range(SUB):
                    r0 = t * TT + ts * 128
                    e = ev % 2
                    ev += 1
                    if isv:
                        act_fn(vst[e], psf[:, ts, :], AF.Copy, [("ps", ts)], [("vst", e)])
                        dma(v_d[r0:r0 + 128, og * 512:(og + 1) * 512], vst[e], [("vst", e)], [])
                    else:
                        gc0 = (og - c.VG // 2) * 512
                        act_fn(gst[e], psf[:, ts, :], AF.Silu, [("ps", ts)], [("gst", e)])
                        tt_op("dve", gst[e], gst[e], hng[:, gc0:gc0 + 512], ALU.mult, [("gst", e), "hng"], [("gst", e)])
                        dma(gg_d[r0:r0 + 128, gc0:gc0 + 512], gst[e], [("gst", e)], [])

    def sweep_stage(d_):
        car = stage()
        kc_t = [car.take([128, RC, 128], BF16) for _ in range(2)]
        vc_t = [car.take([128, RET], BF16) for _ in range(2)]
        Lb = [car.take([128, H, 2, 256], F32) for _ in range(2)]
        ktok = [car.take([128, 256], BF16) for _ in range(2)]
        P.add("dve", lambda e: e.memset(Lb[0], 0.0), [], [("L", 0)])
        order = list(range(NCH)) if d_ == 0 else list(range(NCH - 1, -1, -1))
        cur = 0
        for n, cidx in enumerate(order):
            b = n % 2
            dma(kc_t[b], kT_d.ap().rearrange("c p t -> p c t")[:, :, cidx * 128:(cidx + 1) * 128], [], [("kc", b)])
            dma(vc_t[b], v_d[cidx * 128:(cidx + 1) * 128, :], [], [("vc", b)])
            dma(L_d[d_][cidx], Lb[cur].rearrange("p h a e -> p (h a e)"), [("L", cur)], [])
            nxt = 1 - cur
            for h in range(H):
                if KSW <= 1:
                    break
                hb = h % 2
                pk = ("psb", hb)
                for dc in range(2):
                    if os.environ.get("K_NOTR"):
                        break
                    tr(psb[:, hb, dc * 128:(dc + 1) * 128], kc_t[b][:, 2 * h + dc, :], ident_b, [("kc", b), "identb"], [pk])
                for dc in range(2):
                    if os.environ.get("K_NOTR") or os.environ.get("K_NOEV"):
                        break
                    o = ktok[hb][:, dc * 128:(dc + 1) * 128]
                    evm = os.environ.get("K_EV", "mix")
                    if evm == "dve" or (evm == "mix" and hb == 1):
                        ts_op("dve", o, psb[:, hb, dc * 128:(dc + 1) * 128], kd[d_][:, h:h + 1], None, ALU.mult, None,
                              [pk, "kd"], [("ktok", hb, dc)])
                    else:
                        act_fn(o, psb[:, hb, dc * 128:(dc + 1) * 128], AF.Copy if evm == "copy" else AF.Identity,
                               [pk, "kd"], [("ktok", hb, dc)], scale=kd[d_][:, h:h + 1])
                bank = hb
                if KSW <= 2:
                    continue
                for dc in range(2):
                    bk2 = hb * 2 + dc
                    if os.environ.get("K_MMX") == "n512":
                        mm(psf[:, bk2, 0:512], ktok[hb][:, dc * 128:(dc + 1) * 128],
                           vc_t[b][:, 0:512], True, True, [("ktok", hb, dc), ("vc", b)], [("ps", bk2)])
                    elif os.environ.get("K_MMX") == "ident":
                        mm(psf[:, bk2, 0:256], ident_b,
                           vc_t[b][:, h * 256:(h + 1) * 256], True, True, [("ktok", hb, dc), ("vc", b)], [("ps", bk2)])
                    else:
                        mm(psf[:, bk2, 0:256], ktok[hb][:, dc * 128:(dc + 1) * 128],
                           vc_t[b][:, h * 256:(h + 1) * 256], True, True, [("ktok", hb, 0), ("ktok", hb, 1), ("vc", b)], [("ps", bk2)])
                if KSW <= 3:
                    continue
                for dc in range(2):
                    bk2 = hb * 2 + dc
                    stt_op(Lb[nxt][:, h, dc, :], Lb[cur][:, h, dc, :], gpow[d_][:, 1, h:h + 1],
                           psf[:, bk2, 0:256], ALU.mult, ALU.add,
                           [("L", cur), ("ps", bk2), "gpow"], [("L", nxt)])
            cur = nxt
        dma(ccin_d[:, d_ * H * 512:(d_ + 1) * H * 512], Lb[cur].rearrange("p h a e -> p (h a e)"), [("L", cur)], ["ccin"])

    def exchange_stage():
        car = stage()
        base = 2 * H * 512
        ccv = ccin_d[:, base:base + PC * 16].rearrange("p (c s) -> p c s", s=16)
        pTv = pT_d.ap().rearrange("c p t -> p c t")
        dma(ccv[:, :, 0:8], pTv[:, :, 8:16], [], ["ccin"])
        dma(ccv[:, :, 8:16], pTv[:, :, T:T + 8], [], ["ccin"])
        P.add("pool", lambda e: e.collective_compute("AllGather", ALU.bypass, replica_groups=[list(range(NCORES))],
                                                     ins=[ccin_d.ap()], outs=[ccout_d.ap()]),
              ["ccin"], ["ccout"], cc=True)
        G = [car.take([128, c.CCW], F32) for _ in range(2)]
        Cin = [car.take([128, H * 512], F32) for _ in range(2)]
        halo = car.take([128, 2, PC, 8], F32)
        for d_ in range(2):
            P.add("dve", lambda e, d_=d_: e.memset(Cin[d_], 0.0), [], [("Cin", d_)])
        P.add("dve", lambda e: e.memset(halo, 0.0), [], ["halo"])
        for j in range(NCORES):
            b = j % 2
            dma(G[b], ccout_d[j * 128:(j + 1) * 128, :], ["ccout"], [("G", b)])
            for d_ in range(2):
                for h in range(H):
                    sl = slice(d_ * H * 512 + h * 512, d_ * H * 512 + (h + 1) * 512)
                    stt_op(Cin[d_][:, h * 512:(h + 1) * 512], G[b][:, sl], coef[d_][:, j, h:h + 1],
                           Cin[d_][:, h * 512:(h + 1) * 512], ALU.mult, ALU.add,
                           [("G", b), ("coef", d_, j), ("Cin", d_)], [("Cin", d_)])
            gv = G[b][:, base:base + PC * 16].rearrange("p (c s) -> p c s", s=16)
            stt_op(halo[:, 0, :, :], gv[:, :, 8:16], col(c.C_MASKL + j), halo[:, 0, :, :], ALU.mult, ALU.add,
                   [("G", b), "halo", "cst"], ["halo"])
            stt_op(halo[:, 1, :, :], gv[:, :, 0:8], col(c.C_MASKR + j), halo[:, 1, :, :], ALU.mult, ALU.add,
                   [("G", b), "halo", "cst"], ["halo"])
        for d_ in range(2):
            dma(cin_d[d_], Cin[d_], [("Cin", d_)], [])
        dma(pTv[:, :, 0:8], halo[:, 0, :, :], ["halo"], [])
        dma(pTv[:, :, T + 8:T + 16], halo[:, 1, :, :], ["halo"], [])

    def ret_stage(t, mixT):
        car = stage()
        car.off += MC * TT * 2
        Cin = [car.take([128, H, 512], F32) for _ in range(2)]
        Lc = [car.take([128, H, 512], F32) for _ in range(2)]
        S = [car.take([128, H, 2, 256], BF16) for _ in range(2)]
        qc_t = car.take([128, RC, 128], BF16)
        kc_t = car.take([128, RC, 128], BF16)
        qfb = [car.take([128, RC, 128], BF16) for _ in range(2)]
        vc_t = car.take([128, RET], BF16)
        ggc = car.take([128, RET], F32)
        sTm = [car.take([128, 128], BF16) for _ in range(2)]
        y = car.take([128, RET], BF16)
        ssr = car.take([128, H], F32)
        sst = car.take([128, H], F32)
        junk = car.take([128, 256], BF16)
        for d_ in range(2):
            dma(Cin[d_].rearrange("p h e -> p (h e)"), cin_d[d_], [], [("Cin", d_)])
        qTv = qT_d.ap().rearrange("c p t -> p c t")
        kTv = kT_d.ap().rearrange("c p t -> p c t")
        for cl in range(SUB):
            cidx = t * SUB + cl
            tok = slice(cidx * 128, (cidx + 1) * 128)
            dma(qc_t, qTv[:, :, tok], [], ["qc"])
            dma(kc_t, kTv[:, :, tok], [], ["kc"])
            dma(vc_t, v_d[tok, :], [], ["vc"])
            dma(ggc, gg_d[tok, :], [], ["ggc"])
            for d_ in range(2):
                dma(Lc[d_].rearrange("p h e -> p (h e)"), L_d[d_][cidx], [], [("Lc", d_)])
            for d_ in range(2):
                pw_i = cidx if d_ == 0 else NCH - 1 - cidx
                for h in range(H):
                    stt_op(S[d_][:, h, :, :].rearrange("p a e -> p (a e)"), Cin[d_][:, h, :],
                           gpow[d_][:, pw_i, h:h + 1], Lc[d_][:, h, :], ALU.mult, ALU.add,
                           [("Cin", d_), ("Lc", d_), "gpow"], [("S", d_, h)])
                    for dc in range(2):
                        tt_op("pool", qfb[d_][:, 2 * h + dc, :], qc_t[:, 2 * h + dc, :], qd[d_][:, h, :], ALU.mult,
                              ["qc", "qd"], [("qfb", d_, h)])
            for h in range(H):
                hb = h % 2
                sb_ = 4 + hb
                mm(psf[:, sb_, 0:128], kc_t[:, 2 * h, :], qc_t[:, 2 * h, :], True, False, ["kc", "qc"], [("ps", sb_)])
                mm(psf[:, sb_, 0:128], kc_t[:, 2 * h + 1, :], qc_t[:, 2 * h + 1, :], False, True, ["kc", "qc"],
                   [("ps", sb_)])
                tt_op("dve", sTm[hb], psf[:, sb_, 0:128], DT[:, h, :], ALU.mult, [("ps", sb_), "DT"], [("sTm", hb)])
                ob_ = h % 4
                pk = ("ps", ob_)
                o_ap = psf[:, ob_, 0:256]
                mm(o_ap, sTm[hb], vc_t[:, h * 256:(h + 1) * 256], True, False, [("sTm", hb), "vc"], [pk])
                for d_ in range(2):
                    for dc in range(2):
                        mm(o_ap, qfb[d_][:, 2 * h + dc, :], S[d_][:, h, dc, :], False, (d_ == 1 and dc == 1),
                           [("qfb", d_, h), ("S", d_, h)], [pk])
                act_fn(junk, o_ap, AF.Square, [pk], ["junk", ("ssr", h)], accum=ssr[:, h:h + 1])
                if (h + 1) % 4 == 0 or h == H - 1:
                    h0 = (h // 4) * 4
                    rk = [("ssr", hh) for hh in range(h0, h + 1)]
                    ts_op("dve", sst[:, h0:h + 1], ssr[:, h0:h + 1], 1.0 / 256.0, EPS, ALU.mult, ALU.add, rk, ["sst"])
                    act_fn(sst[:, h0:h + 1], sst[:, h0:h + 1], AF.Sqrt, ["sst"], ["sst"])
                    P.add("dve", lambda e, a=sst[:, h0:h + 1]: e.reciprocal(out=a, in_=a), ["sst"], ["sst"])
                    for hh in range(h0, h + 1):
                        ob2 = hh % 4
                        stt_op(y[:, hh * 256:(hh + 1) * 256], psf[:, ob2, 0:256], sst[:, hh:hh + 1],
                               ggc[:, hh * 256:(hh + 1) * 256], ALU.mult, ALU.mult,
                               [("ps", ob2), "sst", "ggc"], [("y", hh)])
                        tb_ = hh % 2
                        pk2 = ("psb", tb_)
                        for dc in range(2):
                            tr(psb[:, tb_, dc * 128:(dc + 1) * 128], y[:, hh * 256 + dc * 128: hh * 256 + (dc + 1) * 128],
                               ident_b, [("y", hh), "identb"], [pk2])
                        for dc in range(2):
                            copy_op("act" if tb_ == 0 else "dve", mixT[:, 2 * hh + dc, cl * 128:(cl + 1) * 128],
                                    psb[:, tb_, dc * 128:(dc + 1) * 128], [pk2], [("mixT", 2 * hh + dc)])

    def pool_wout_stage(t, mixT):
        car = stage()
        car.off += MC * TT * 2
        W = TT + 16
        pext = car.take([128, PC, W], F32)
        pw = car.take([128, 4, PGC, PG], BF16)
        vext = car.take([128, W], F32)
        cs = [car.take([128, W], F32) for _ in range(2)]
        inv = car.take([128, 4, TT], F32)
        sa = [car.take([128, W], F32) for _ in range(4)]
        pooled = car.take([128, PC, TT], BF16)
        ring = WRing(car, 4)
        xres = [car.take([128, 512], F32) for _ in range(2)]
        ost = [car.take([128, 512], F32) for _ in range(2)]
        dma(pext, pT_d.ap().rearrange("c p t -> p c t")[:, :, t * TT: t * TT + W], [], ["pext"])
        dma(pw.rearrange("p g c d -> p (g c d)"), pwS.ap(), [], ["pw"])
        P.add("dve", lambda e: e.memset(vext, 1.0), [], ["vext"])
        if t == 0:
            ts_op("dve", vext[:, 0:8], vext[:, 0:8], col(c.C_VL), None, ALU.mult, None, ["vext", "cst"], ["vext"])
        if t == NT - 1:
            ts_op("dve", vext[:, TT + 8:W], vext[:, TT + 8:W], col(c.C_VR), None, ALU.mult, None, ["vext", "cst"], ["vext"])
        for gi in range(4):
            w = 2 << gi
            lo = w // 2
            src, span, n = vext, 1, 0
            skey = "vext"
            while span < w:
                dst = cs[n % 2]
                tt_op("dve", dst[:, 0:W - span], src[:, 0:W - span], src[:, span:W], ALU.add, [skey], [("cs", n % 2)])
                src, skey = dst, ("cs", n % 2)
                span *= 2
                n += 1
            P.add("dve", lambda e, o=inv[:, gi, :], i=src[:, 8 - lo: 8 - lo + TT]: e.reciprocal(out=o, in_=i),
                  [skey], [("inv", gi)])
        for pc in range(PC):
            gi = pc // PGC
            w = 2 << gi
            lo = w // 2
            eng = "dve" if pc % 2 == 0 else "pool"
            o2 = (pc % 2) * 2
            src, span, n = pext[:, pc, :], 1, 0
            skey = "pext"
            while span < w:
                dst = sa[o2 + n % 2]
                tt_op(eng, dst[:, 0:W - span], src[:, 0:W - span], src[:, span:W], ALU.add, [skey], [("sa", o2 + n % 2)])
                src, skey = dst, ("sa", o2 + n % 2)
                span *= 2
                n += 1
            dst = sa[o2 + n % 2]
            tt_op(eng, dst[:, 0:TT], src[:, 8 - lo: 8 - lo + TT], inv[:, gi, :], ALU.mult, [skey, ("inv", gi)],
                  [("sa", o2 + n % 2)])
            tt_op(eng, pooled[:, pc, :], dst[:, 0:TT], pext[:, pc, 8:8 + TT], ALU.subtract, [("sa", o2 + n % 2), "pext"],
                  [("pooled", pc)])
        nb = 0
        for gi in range(4):
            for dcol in range(PGC):
                bank = 4 + nb % 2
                nb += 1
                for cc_ in range(PGC):
                    mm(psf[:, bank, 0:TT], pw[:, gi, cc_, dcol * 128:(dcol + 1) * 128], pooled[:, gi * PGC + cc_, :],
                       cc_ == 0, cc_ == PGC - 1, ["pw", ("pooled", gi * PGC + cc_)], [("ps", bank)])
                oc_ = gi * PGC + dcol
                act_fn(mixT[:, RC + oc_, :], psf[:, bank, 0:TT], AF.Copy, [("ps", bank), "smp"], [("mixT", RC + oc_)],
                       scale=smp[:, c.S_PSC + oc_: c.S_PSC + oc_ + 1])
        pieces = []
        for og in range(OG):
            f0 = 0
            while f0 < MC:
                fn = min(8, MC - f0)
                pieces.append((og, f0, fn))
                f0 += fn
        slots = {}
        PF = 2

        def issue(n):
            if n < len(pieces):
                og, f0, fn = pieces[n]
                b, buf = ring.next()
                bv = buf.rearrange("p (a b) -> p a b", b=512)
                dma(bv[:, 0:fn, :], woutS[og, :, f0:f0 + fn, :], [], [("wb", b)])
                slots[n] = (b, bv)

        for n in range(PF):
            issue(n)
        ev = 0
        for n, (og, f0, fn) in enumerate(pieces):
            issue(n + PF)
            b, bv = slots.pop(n)
            for fi in range(fn):
                f = f0 + fi
                for ts in range(SUB):
                    mm(psf[:, ts, :], mixT[:, f, ts * 128:(ts + 1) * 128], bv[:, fi, :], f == 0, f == MC - 1,
                       [("wb", b), ("mixT", f)], [("ps", ts)])
            if f0 + fn == MC:
                for ts in range(SUB):
                    r0 = t * TT + ts * 128
                    e = ev % 2
                    ev += 1
                    dma(xres[e], x1_d[r0:r0 + 128, og * 512:(og + 1) * 512], [], [("xres", e)])
                    tt_op("dve", ost[e], psf[:, ts, :], xres[e], ALU.add, [("ps", ts), ("xres", e)], [("ost", e)])
                    dma(x2_d[r0:r0 + 128, og * 512:(og + 1) * 512], ost[e], [("ost", e)], [])

    def final_stage(t):
        car = stage()
        gfin = car.take([128, D], F32)
        xrow = [car.take([128, D], F32) for _ in range(2)]
        junk = car.take([128, D], BF16)
        ss = car.take([128, SUB], F32)
        sst = car.take([128, SUB], F32)
        dma(gfin, fin_d.ap().partition_broadcast(128).rearrange("p o n -> p (o n)"), [], ["gfin"])
        for ts in range(SUB):
            r0 = t * TT + ts * 128
            b = ts % 2
            dma(xrow[b], out_d[r0:r0 + 128, :], [], [("xrow", b)])
            act_fn(junk, xrow[b], AF.Square, [("xrow", b)], ["junk", ("ss", ts)], accum=ss[:, ts:ts + 1])
            rstd_from_ss(ss[:, ts:ts + 1], sst[:, ts:ts + 1], 1, 1.0 / D, ("ss", ts))
            stt_op(xrow[b], xrow[b], ss[:, ts:ts + 1], gfin, ALU.mult, ALU.mult, [("xrow", b), ("ss", ts), "gfin"],
                   [("xrow", b)])
            dma(out_d[r0:r0 + 128, :], xrow[b], [("xrow", b)], [])

    import os
    LV = int(os.environ.get("K_LEVEL", "99"))
    if LV == 1:
        for t in range(NT):
            ffn_stage(t, x_d, g1_d, 0, out_d)
    if LV >= 2:
        for t in range(NT):
            ffn_stage(t, x_d, g1_d, 0, x1_d)
            win_stage(t)
    if LV >= 3:
        sweep_stage(0)
        sweep_stage(1)
    if LV >= 4:
        exchange_stage()
    if LV >= 5:
        for t in range(NT):
            mixT = Carver(arena, PERS, ARENA_BYTES).take([128, MC, TT], BF16)
            ret_stage(t, mixT)
            if LV >= 6:
                pool_wout_stage(t, mixT)
            if LV >= 7:
                ffn_stage(t, x2_d, g2_d, 1, out_d)
            if LV >= 8:
                final_stage(t)
    P.barrier()
    P.add("sp", None)
    P.add("act", None)
    P.add("dve", None)
    P.add("pool", None)
    P.add("pe", None)

    import contextlib
    with contextlib.ExitStack() as es:
        esem = {e: es.enter_context(nc.semaphore("sem_" + e)) for e in Prog.ENGS}
        dsems = [es.enter_context(nc.semaphore("dsem%d" % i)) for i in range(NDS)]
        ccsem = es.enter_context(nc.semaphore("ccsem"))
        block = es.enter_context(nc.Block())
        P.emit(nc, block, esem, dsems, ccsem)
    return nc, len(P.ops)


def make_consts(cfg, core):
    c = cfg
    a = np.zeros((128, c.NCST), np.float32)
    j = np.arange(128, dtype=np.float32)[:, None]
    i = np.arange(128, dtype=np.float32)[None, :]
    a[:, c.C_RELP:c.C_RELP + 128] = np.maximum(i - j, 0)
    a[:, c.C_RELN:c.C_RELN + 128] = np.maximum(j - i, 0)
    a[:, c.C_MGE:c.C_MGE + 128] = (i >= j)
    a[:, c.C_MGT:c.C_MGT + 128] = (j > i)
    a[:, c.C_IP1:c.C_IP1 + 128] = i + 1
    a[:, c.C_128MI:c.C_128MI + 128] = 128 - i
    a[:, c.C_IDENT:c.C_IDENT + 128] = np.eye(128, dtype=np.float32)
    a[:, c.C_127MJ] = 127 - j[:, 0]
    a[:, c.C_J] = j[:, 0]
    fr = (1.0 / (np.float32(10000.0) ** (np.arange(0, 256, 2, dtype=np.float32) / np.float32(256)))).astype(np.float32)
    a[:, c.C_FREQ] = fr
    a[:, c.C_VL] = 1.0 if core > 0 else 0.0
    a[:, c.C_VR] = 1.0 if core < NCORES - 1 else 0.0
    for jj in range(NCORES):
        a[:, c.C_MASKL + jj] = 1.0 if jj == core - 1 else 0.0
        a[:, c.C_MASKR + jj] = 1.0 if jj == core + 1 else 0.0
        a[:, c.C_EXPF + jj] = float(c.T * (core - 1 - jj)) if jj < core else 0.0
        a[:, c.C_MSKF + jj] = 1.0 if jj < core else 0.0
        a[:, c.C_EXPB + jj] = float(c.T * (jj - core - 1)) if jj > core else 0.0
        a[:, c.C_MSKB + jj] = 1.0 if jj > core else 0.0
    return a


def run(cfg, inputs, trace=False):
    c = cfg
    f = lambda k: np.ascontiguousarray(np.asarray(inputs[k]))
    x = f("x").reshape(c.SEQ, c.D)
    pos = f("positions").reshape(1, c.SEQ).astype(np.int32)
    smallp = np.zeros((128, c.NSP), np.float32)
    smallp[:, c.S_G1:c.S_G1 + c.KC] = f("ffn1_norm").reshape(c.KC, 128).T
    smallp[:, c.S_GM:c.S_GM + c.KC] = f("mix_norm").reshape(c.KC, 128).T
    smallp[:, c.S_G2:c.S_G2 + c.KC] = f("ffn2_norm").reshape(c.KC, 128).T
    smallp[:, c.S_PSC:c.S_PSC + c.PC] = f("pool_scale").reshape(c.PC, 128).T
    smallp[:, c.S_LF:c.S_LF + c.H] = f("ret_decay_fwd").reshape(1, c.H)
    smallp[:, c.S_LB:c.S_LB + c.H] = f("ret_decay_bwd").reshape(1, c.H)
    shared = {
        "smallp": smallp,
        "hn_gain": f("ret_head_norm").reshape(1, c.RET),
        "fin_gain": f("final_norm").reshape(1, c.D),
        "g1_row": f("ffn1_norm").reshape(1, c.D), "gm_row": f("mix_norm").reshape(1, c.D),
        "g2_row": f("ffn2_norm").reshape(1, c.D),
        "w1g": f("ffn1_w_gate").reshape(c.D, c.FF), "w1u": f("ffn1_w_up").reshape(c.D, c.FF),
        "w1d": f("ffn1_w_down").reshape(c.FF, c.D),
        "w2g": f("ffn2_w_gate").reshape(c.D, c.FF), "w2u": f("ffn2_w_up").reshape(c.D, c.FF),
        "w2d": f("ffn2_w_down").reshape(c.FF, c.D),
        "w_in": f("w_in").reshape(c.D, c.PROJ), "w_out": f("w_out").reshape(c.D, c.D),
        "pool_w": f("pool_w").reshape(4 * c.PG, c.PG),
    }
    nc, nops = build(c)
    in_maps = []
    for i in range(NCORES):
        m = dict(shared)
        m["x"] = np.ascontiguousarray(x[i * c.T:(i + 1) * c.T])
        m["pos"] = np.ascontiguousarray(pos[:, i * c.T:(i + 1) * c.T])
        m["cst"] = make_consts(c, i)
        in_maps.append(m)
    res = run_bass_kernel_spmd(nc, in_maps, core_ids=list(range(NCORES)), trace=trace)
    out = np.concatenate([np.asarray(r["out"]) for r in res.results], axis=0)
    return out.reshape(1, c.SEQ, c.D).astype(np.float32), res


def kernel(**inputs):
    cfg = Cfg(4096, 11008, 16384, 8)
    out, _ = run(cfg, inputs)
    return out
```
